# Optimizing a Trainium2 kernel written in Bass

```python
import jax, jax.numpy as jnp
from jax import lax
import numpy as np

D_MODEL = 1024
BATCH = 16
SEQ = 2048
DEPTH = 4
DEC_BATCH = 128
DEC_SEQ = 4
PAST_LEN = 8192
PAGE_SIZE = 128

N_EVEN = (DEPTH + 1) // 2
N_ODD = DEPTH // 2
D_POOL = D_MODEL // 2
POOL_WINDOWS = (2, 4, 8, 16)
POOL_GROUPS = len(POOL_WINDOWS)
POOL_CH = D_POOL // POOL_GROUPS
POOL_BUF = max(POOL_WINDOWS) - 1
D_SCONV = D_MODEL // 2
SCONV_W = 3
D_IN_EVEN = D_POOL + 3 * D_SCONV
MLA_HEADS = 16
QK_NOPE = 128
QK_ROPE = 64
V_HEAD = 128
Q_LORA = 384
KV_LORA = 256
D_IN_ODD = Q_LORA + KV_LORA + QK_ROPE
ROPE_THETA = 10000.0
ATTN_SCALE = (QK_NOPE + QK_ROPE) ** -0.5
Q_BLOCK = 128
D_FF = 2816
FFN_CONV_W = 3
NORM_EPS = 1e-6

kernel_name = 'hybrid_pool_sconv_mla_convffn_step'


def rmsnorm(x, g):
    xf = x.astype(jnp.float32)
    inv = lax.rsqrt(jnp.mean(xf * xf, axis=-1, keepdims=True) + NORM_EPS)
    return (xf * inv * g.astype(jnp.float32)).astype(x.dtype)


def rope(x, pos):
    half = x.shape[-1] // 2
    freqs = ROPE_THETA ** (-jnp.arange(half, dtype=jnp.float32) / half)
    ang = pos.astype(jnp.float32)[:, None] * freqs[None, :]
    shape = (pos.shape[0],) + (1,) * (x.ndim - 3) + (half,)
    cos = jnp.cos(ang).reshape(shape)
    sin = jnp.sin(ang).reshape(shape)
    xf = x.astype(jnp.float32)
    x1, x2 = xf[..., :half], xf[..., half:]
    return jnp.concatenate([x1 * cos - x2 * sin, x2 * cos + x1 * sin], axis=-1).astype(x.dtype)


def multiscale_pool(u_ext, n_new):
    L = u_ext.shape[1]
    uf = u_ext.astype(jnp.float32)
    cs0 = jnp.pad(jnp.cumsum(uf, axis=1), ((0, 0), (1, 0), (0, 0), (0, 0)))
    row = jnp.arange(L - n_new, L)
    outs = []
    for g, w in enumerate(POOL_WINDOWS):
        csg = jnp.pad(cs0[:, :, g], ((0, 0), (w, 0), (0, 0)))
        total = csg[:, L - n_new + 1 + w: L + 1 + w] - csg[:, L - n_new + 1: L + 1]
        count = jnp.minimum(row + 1, w).astype(jnp.float32)
        outs.append(total / count[None, :, None] - uf[:, L - n_new:, g])
    return jnp.stack(outs, axis=2).astype(u_ext.dtype)


def causal_dwconv(v_ext, w, n_new):
    K = w.shape[0]
    off = v_ext.shape[1] - n_new - (K - 1)
    out = w[0] * v_ext[:, off: off + n_new]
    for k in range(1, K):
        out = out + w[k] * v_ext[:, off + k: off + k + n_new]
    return out


def even_mixer(xn, w_in, w_map, scale, w_conv, w_out, pool_prev, conv_prev):
    b, t, _ = xn.shape
    h = xn @ w_in
    u = h[..., :D_POOL]
    gate_b, gate_c, hx = jnp.split(h[..., D_POOL:], 3, axis=-1)
    u_ext = jnp.concatenate([pool_prev, u], axis=1)
    d = multiscale_pool(u_ext.reshape(b, -1, POOL_GROUPS, POOL_CH), t)
    y_pool = jnp.einsum('btgc,gcd->btgd', d, w_map).reshape(b, t, D_POOL) * scale
    v_ext = jnp.concatenate([conv_prev, gate_c * hx], axis=1)
    y_conv = gate_b * causal_dwconv(v_ext, w_conv, t)
    y = jnp.concatenate([y_pool, y_conv], axis=-1) @ w_out
    return y, u_ext[:, -POOL_BUF:], v_ext[:, -(SCONV_W - 1):]


def latent_attend(q_lat, q_rope, k_lat, k_rope, q_pos, k_pos):
    s = (jnp.einsum('bqhr,bkr->bhqk', q_lat, k_lat)
         + jnp.einsum('bqhp,bkp->bhqk', q_rope, k_rope)).astype(jnp.float32) * ATTN_SCALE
    mask = (k_pos[None, :] <= q_pos[:, None])[None, None]
    s = jnp.where(mask, s, jnp.finfo(jnp.float32).min)
    p = jax.nn.softmax(s, axis=-1).astype(k_lat.dtype)
    return jnp.einsum('bhqk,bkr->bqhr', p, k_lat)


def mla_mixer(xn, past_ckv, past_kr, w_in, q_norm, kv_norm, w_uq, w_uk, w_uv, w_out):
    b, t, _ = xn.shape
    p_len = past_ckv.shape[1]
    q_pos = p_len + jnp.arange(t)
    k_pos = jnp.arange(p_len + t)
    h = xn @ w_in
    c_q = rmsnorm(h[..., :Q_LORA], q_norm)
    c_kv = rmsnorm(h[..., Q_LORA:Q_LORA + KV_LORA], kv_norm)
    k_rope = rope(h[..., Q_LORA + KV_LORA:], q_pos)
    q = jnp.einsum('btr,rhd->bthd', c_q, w_uq)
    q_rope = rope(q[..., QK_NOPE:], q_pos)
    q_lat = jnp.einsum('bthd,rhd->bthr', q[..., :QK_NOPE], w_uk)
    keys_c = jnp.concatenate([past_ckv, c_kv], axis=1)
    keys_r = jnp.concatenate([past_kr, k_rope], axis=1)
    qb = Q_BLOCK if t % Q_BLOCK == 0 else t
    nb = t // qb

    def block(args):
        ql, qr, qp = args
        return latent_attend(ql, qr, keys_c, keys_r, qp, k_pos)

    blocks = (q_lat.reshape(b, nb, qb, MLA_HEADS, KV_LORA).swapaxes(0, 1),
              q_rope.reshape(b, nb, qb, MLA_HEADS, QK_ROPE).swapaxes(0, 1),
              q_pos.reshape(nb, qb))
    o_lat = lax.map(block, blocks).swapaxes(0, 1).reshape(b, t, MLA_HEADS, KV_LORA)
    o = jnp.einsum('bthr,rhv->bthv', o_lat, w_uv).reshape(b, t, MLA_HEADS * V_HEAD)
    return o @ w_out, c_kv, k_rope


def conv_ffn(xn, w_up, w_conv, w_down, prev):
    t = xn.shape[1]
    gate, up = jnp.split(xn @ w_up, 2, axis=-1)
    g_ext = jnp.concatenate([prev, gate], axis=1)
    act = jax.nn.silu(causal_dwconv(g_ext, w_conv, t))
    return (act * up) @ w_down, g_ext[:, -(FFN_CONV_W - 1):]


def run_trunk(x, pool_prev, conv_prev, past_ckv, past_kr, ffn_prev, prm):
    new_pool, new_conv, new_ckv, new_kr, new_ffn = [], [], [], [], []
    for i in range(DEPTH):
        g = prm['norms'][i]
        xn = rmsnorm(x, g[0])
        if i % 2 == 0:
            e = i // 2
            y, sp, sc = even_mixer(xn, prm['w_in_even'][e], prm['w_pool_map'][e], prm['pool_scale'][e],
                                   prm['w_sconv'][e], prm['w_out_even'][e], pool_prev[e], conv_prev[e])
            new_pool.append(sp)
            new_conv.append(sc)
        else:
            o = i // 2
            y, ck, kr = mla_mixer(xn, past_ckv[o], past_kr[o], prm['w_in_odd'][o], prm['q_norm'][o],
                                  prm['kv_norm'][o], prm['w_uq'][o], prm['w_uk'][o], prm['w_uv'][o],
                                  prm['w_out_odd'][o])
            new_ckv.append(ck)
            new_kr.append(kr)
        x = x + rmsnorm(y, g[1])
        y, sf = conv_ffn(rmsnorm(x, g[2]), prm['w_ffn_up'][i], prm['w_ffn_conv'][i], prm['w_ffn_down'][i], ffn_prev[i])
        new_ffn.append(sf)
        x = x + rmsnorm(y, g[3])
    return (x, jnp.stack(new_pool), jnp.stack(new_conv), jnp.stack(new_ckv), jnp.stack(new_kr), jnp.stack(new_ffn))


def setup_inputs(seed: int = 0) -> dict:
    key = jax.random.key(seed)
    ks = jax.random.split(key, 32)
    f32 = jnp.float32
    n_pages = PAST_LEN // PAGE_SIZE
    n_pool = (5 * DEC_BATCH * n_pages) // 4

    def nrm(k, shape, scale=1.0):
        return jax.random.normal(k, shape, f32) * scale

    page_table = jax.random.permutation(ks[7], n_pool)[: DEC_BATCH * n_pages].reshape(DEC_BATCH, n_pages).astype(jnp.int32)
    return {
        'x_prompt': nrm(ks[0], (BATCH, SEQ, D_MODEL)),
        'x_sample': nrm(ks[1], (DEC_BATCH, DEC_SEQ, D_MODEL)),
        'state_pool': nrm(ks[2], (N_EVEN, DEC_BATCH, POOL_BUF, D_POOL)),
        'state_sconv': nrm(ks[3], (N_EVEN, DEC_BATCH, SCONV_W - 1, D_SCONV)),
        'cache_ckv': nrm(ks[4], (N_ODD, n_pool, PAGE_SIZE, KV_LORA)),
        'cache_krope': nrm(ks[5], (N_ODD, n_pool, PAGE_SIZE, QK_ROPE)),
        'state_ffn': nrm(ks[6], (DEPTH, DEC_BATCH, FFN_CONV_W - 1, D_FF)),
        'page_table': page_table,
        'norms': 1.0 + nrm(ks[8], (DEPTH, 4, D_MODEL), 0.1),
        'w_in_even': nrm(ks[9], (N_EVEN, D_MODEL, D_IN_EVEN), D_MODEL ** -0.5),
        'w_pool_map': nrm(ks[10], (N_EVEN, POOL_GROUPS, POOL_CH, POOL_CH), POOL_CH ** -0.5),
        'pool_scale': 1.0 + nrm(ks[11], (N_EVEN, D_POOL), 0.1),
        'w_sconv': nrm(ks[12], (N_EVEN, SCONV_W, D_SCONV), SCONV_W ** -0.5),
        'w_out_even': nrm(ks[13], (N_EVEN, D_POOL + D_SCONV, D_MODEL), (D_POOL + D_SCONV) ** -0.5),
        'w_in_odd': nrm(ks[14], (N_ODD, D_MODEL, D_IN_ODD), D_MODEL ** -0.5),
        'q_norm': 1.0 + nrm(ks[15], (N_ODD, Q_LORA), 0.1),
        'kv_norm': 1.0 + nrm(ks[16], (N_ODD, KV_LORA), 0.1),
        'w_uq': nrm(ks[17], (N_ODD, Q_LORA, MLA_HEADS, QK_NOPE + QK_ROPE), Q_LORA ** -0.5),
        'w_uk': nrm(ks[18], (N_ODD, KV_LORA, MLA_HEADS, QK_NOPE), KV_LORA ** -0.5),
        'w_uv': nrm(ks[19], (N_ODD, KV_LORA, MLA_HEADS, V_HEAD), KV_LORA ** -0.5),
        'w_out_odd': nrm(ks[20], (N_ODD, MLA_HEADS * V_HEAD, D_MODEL), (MLA_HEADS * V_HEAD) ** -0.5),
        'w_ffn_up': nrm(ks[21], (DEPTH, D_MODEL, 2 * D_FF), D_MODEL ** -0.5),
        'w_ffn_conv': nrm(ks[22], (DEPTH, FFN_CONV_W, D_FF), FFN_CONV_W ** -0.5),
        'w_ffn_down': nrm(ks[23], (DEPTH, D_FF, D_MODEL), D_FF ** -0.5),
    }


def reference(x_prompt, x_sample, state_pool, state_sconv, cache_ckv, cache_krope, state_ffn, page_table,
              norms, w_in_even, w_pool_map, pool_scale, w_sconv, w_out_even, w_in_odd, q_norm, kv_norm,
              w_uq, w_uk, w_uv, w_out_odd, w_ffn_up, w_ffn_conv, w_ffn_down):
    prm = {'norms': norms, 'w_in_even': w_in_even, 'w_pool_map': w_pool_map, 'pool_scale': pool_scale,
           'w_sconv': w_sconv, 'w_out_even': w_out_even, 'w_in_odd': w_in_odd, 'q_norm': q_norm,
           'kv_norm': kv_norm, 'w_uq': w_uq, 'w_uk': w_uk, 'w_uv': w_uv, 'w_out_odd': w_out_odd,
           'w_ffn_up': w_ffn_up, 'w_ffn_conv': w_ffn_conv, 'w_ffn_down': w_ffn_down}
    dt = x_prompt.dtype
    bp = x_prompt.shape[0]
    y_prompt, pool_p, sconv_p, ckv_p, kr_p, ffn_p = run_trunk(
        x_prompt,
        [jnp.zeros((bp, 0, D_POOL), dt)] * N_EVEN,
        [jnp.zeros((bp, SCONV_W - 1, D_SCONV), dt)] * N_EVEN,
        [jnp.zeros((bp, 0, KV_LORA), dt)] * N_ODD,
        [jnp.zeros((bp, 0, QK_ROPE), dt)] * N_ODD,
        [jnp.zeros((bp, FFN_CONV_W - 1, D_FF), dt)] * DEPTH,
        prm)
    db = x_sample.shape[0]
    past_len = page_table.shape[1] * cache_ckv.shape[2]
    past_ckv = [cache_ckv[o][page_table].reshape(db, past_len, KV_LORA) for o in range(N_ODD)]
    past_kr = [cache_krope[o][page_table].reshape(db, past_len, QK_ROPE) for o in range(N_ODD)]
    y_sample, pool_s, sconv_s, ckv_s, kr_s, ffn_s = run_trunk(
        x_sample,
        [state_pool[e] for e in range(N_EVEN)],
        [state_sconv[e] for e in range(N_EVEN)],
        past_ckv, past_kr,
        [state_ffn[i] for i in range(DEPTH)],
        prm)
    return (y_prompt, y_sample, pool_p, pool_s, sconv_p, sconv_s, ckv_p, kr_p, ckv_s, kr_s, ffn_p, ffn_s)
```

```python
from contextlib import ExitStack
import numpy as np
import concourse.bass as bass
import concourse.mybir as mybir
from concourse.bass_utils import run_bass_kernel_spmd

F32 = mybir.dt.float32
BF16 = mybir.dt.bfloat16
I32 = mybir.dt.int32
AF = mybir.ActivationFunctionType
ALU = mybir.AluOpType

ENGS = ('pe', 'act', 'dve', 'pool', 'sp')


class Buf:
    __slots__ = ('name', 'last_write', 'reads', 'dsem')

    def __init__(self, name=''):
        self.name = name
        self.last_write = None
        self.reads = []
        self.dsem = None


class DSem:
    __slots__ = ('sem', 'count')

    def __init__(self, sem):
        self.sem = sem
        self.count = 0


class Op:
    __slots__ = ('eng', 'fn', 'deps', 'is_dma', 'dsem', 'val', 'needed', 'idx')

    def __init__(self, eng, fn):
        self.eng = eng
        self.fn = fn
        self.deps = []
        self.is_dma = False
        self.dsem = None
        self.val = None
        self.needed = False
        self.idx = 0


class Sched:
    def __init__(self, nc, stack):
        self.nc = nc
        self.stack = stack
        self.ops = {e: [] for e in ENGS}
        self.esem = {e: stack.enter_context(nc.semaphore('es_' + e)) for e in ENGS}
        self.all_stores = []
        self.nops = 0

    def new_dsem(self, name):
        return DSem(self.stack.enter_context(self.nc.semaphore('ds_' + name)))

    def buf(self, name='', dma=False):
        b = Buf(name)
        if dma:
            b.dsem = self.new_dsem(name)
        return b

    def op(self, eng, fn, reads=(), writes=(), dma=None, store=False):
        o = Op(eng, fn)
        self.nops += 1
        o.idx = self.nops
        if dma is not None:
            o.is_dma = True
            o.dsem = dma
        deps = {}

        def add(d, waw=False):
            if d is None:
                return
            if waw and o.is_dma and d.is_dma and d.dsem is o.dsem:
                return
            if o.eng == 'pe' and d.eng == 'pe' and not d.is_dma and not o.is_dma:
                return
            deps[id(d)] = d

        for b in reads:
            add(b.last_write)
        for b in writes:
            add(b.last_write, waw=True)
            for r in b.reads:
                add(r)
        o.deps = []
        for d in deps.values():
            if d.is_dma:
                o.deps.append((d.dsem.sem, d.dsem.count))
            else:
                d.needed = True
                o.deps.append(d)
        if dma is not None:
            dma.count += 16
            o.val = dma.count
        for b in reads:
            b.reads.append(o)
        for b in writes:
            b.last_write = o
            b.reads = []
        self.ops[eng].append(o)
        if store:
            self.all_stores.append(o)
        return o

    def finish(self):
        fin = Op('sp', None)
        fin.deps = [(d.dsem.sem, d.dsem.count) for d in self.all_stores]
        self.ops['sp'].append(fin)
        for e in ENGS:
            c = 0
            for o in self.ops[e]:
                if not o.is_dma and o.fn is not None and o.needed:
                    c += 1
                    o.val = c

    def emit(self, block):
        esem = self.esem

        def run(ename, eng):
            waited = {}
            for o in self.ops[ename]:
                need = {}
                for d in o.deps:
                    if isinstance(d, tuple):
                        s, v = d
                    else:
                        s, v = esem[d.eng], d.val
                    k = id(s)
                    if k not in need or need[k][1] < v:
                        need[k] = (s, v)
                for k, (s, v) in need.items():
                    if waited.get(k, 0) >= v:
                        continue
                    eng.wait_ge(s, v)
                    waited[k] = v
                if o.fn is None:
                    continue
                ins = o.fn(eng)
                if o.is_dma:
                    ins.then_inc(o.dsem.sem, 16)
                elif o.needed:
                    ins.then_inc(esem[ename], 1)

        @block.tensor
        def _(eng):
            run('pe', eng)

        @block.scalar
        def _(eng):
            run('act', eng)

        @block.vector
        def _(eng):
            run('dve', eng)

        @block.gpsimd
        def _(eng):
            run('pool', eng)

        @block.sync
        def _(eng):
            run('sp', eng)


class Cfg:
    D = 1024
    KD = 8
    DFF = 2816
    KF = 22
    DEPTH = 4
    S = 2048
    NSEQ = 2
    NB = 16
    DEC = 4
    NPG = 64
    PAGE = 128
    NPOOLPG = 10240
    W = 512
    NCORES = 8
    EPS = 1e-6
    POOL_W = (2, 4, 8, 16)
    HEADS = 16
    NSLOT = 6

    @property
    def WS(self):
        return self.NB * self.DEC

    @property
    def PAST(self):
        return self.NPG * self.PAGE


def kmajor(w):
    K, N = w.shape
    kc = K // 128
    return np.ascontiguousarray(w.reshape(kc, 128, N).transpose(1, 0, 2).reshape(128, kc * N))


def build_weight_stream(inp):
    pieces = []
    for i in range(4):
        if i % 2 == 0:
            e = i // 2
            w = inp['w_in_even'][e]
            pieces.append((f'E_u{e}', kmajor(w[:, 0:512])))
            pieces.append((f'E_hx{e}', kmajor(w[:, 1536:2048])))
            pieces.append((f'E_gc{e}', kmajor(w[:, 1024:1536])))
            pieces.append((f'E_gb{e}', kmajor(w[:, 512:1024])))
            pieces.append((f'E_map{e}', np.ascontiguousarray(inp['w_pool_map'][e].transpose(1, 0, 2).reshape(128, 512))))
            wo = inp['w_out_even'][e]
            pieces.append((f'E_out{e}_0', kmajor(wo[:, 0:512])))
            pieces.append((f'E_out{e}_1', kmajor(wo[:, 512:1024])))
        else:
            o = i // 2
            w = inp['w_in_odd'][o]
            pieces.append((f'O_q{o}', kmajor(w[:, 0:384])))
            pieces.append((f'O_kv{o}', kmajor(w[:, 384:704])))
            uq = inp['w_uq'][o]
            for g in range(4):
                nope = uq[:, 4 * g:4 * g + 4, 0:128].reshape(384, 512)
                r = uq[:, 4 * g:4 * g + 4, 128:192]
                rp = np.concatenate([r, r[:, :, 32:64], r[:, :, 0:32]], axis=2).reshape(384, 512)
                pieces.append((f'O_uq{o}_{g}', np.concatenate([kmajor(nope), kmajor(rp)], axis=1)))
            pieces.append((f'O_uk{o}', np.ascontiguousarray(inp['w_uk'][o].transpose(2, 1, 0).reshape(128, 4096))))
            pieces.append((f'O_uv{o}', kmajor(inp['w_uv'][o].reshape(256, 2048))))
            wo = inp['w_out_odd'][o]
            for g in range(4):
                pieces.append((f'O_out{o}_{g}', kmajor(wo[:, g * 256:(g + 1) * 256])))
        wu = inp['w_ffn_up'][i]
        for j in range(11):
            pieces.append((f'F_up{i}_{j}', np.concatenate(
                [kmajor(wu[:, j * 256:(j + 1) * 256]), kmajor(wu[:, 2816 + j * 256:2816 + (j + 1) * 256])], axis=1)))
        wd = inp['w_ffn_down'][i]
        for c in range(8):
            pieces.append((f'F_dn{i}_{c}', kmajor(wd[:, c * 128:(c + 1) * 128])))
    table = {}
    off = 0
    for name, a in pieces:
        table[name] = (off, a.shape[1])
        off += a.shape[1]
    wst = np.concatenate([a for _, a in pieces], axis=1).astype(np.float32)
    return wst, table


def piece_table():
    cols = []
    for i in range(4):
        if i % 2 == 0:
            e = i // 2
            cols += [(f'E_u{e}', 4096), (f'E_hx{e}', 4096), (f'E_gc{e}', 4096), (f'E_gb{e}', 4096),
                     (f'E_map{e}', 512), (f'E_out{e}_0', 4096), (f'E_out{e}_1', 4096)]
        else:
            o = i // 2
            cols += [(f'O_q{o}', 3072), (f'O_kv{o}', 2560)]
            cols += [(f'O_uq{o}_{g}', 3072) for g in range(4)]
            cols += [(f'O_uk{o}', 4096), (f'O_uv{o}', 4096)]
            cols += [(f'O_out{o}_{g}', 4096) for g in range(4)]
        cols += [(f'F_up{i}_{j}', 4096) for j in range(11)]
        cols += [(f'F_dn{i}_{c}', 2816) for c in range(8)]
    table = {}
    off = 0
    for n, c in cols:
        table[n] = (off, c)
        off += c
    return table, off


def param_layout():
    lay = {}
    off = 0
    lay['norm'] = off; off += 4 * 4 * 8
    lay['pscale'] = off; off += 2 * 4
    lay['sconv'] = off; off += 2 * 3 * 4
    lay['fconv'] = off; off += 4 * 3 * 22
    lay['qnorm'] = off; off += 2 * 3
    return lay, off


def build_params(inp):
    lay, n = param_layout()
    pr = np.zeros((128, n), np.float32)
    fm = lambda v: v.reshape(-1, 128).T
    for i in range(4):
        for j in range(4):
            pr[:, lay['norm'] + (i * 4 + j) * 8: lay['norm'] + (i * 4 + j) * 8 + 8] = fm(inp['norms'][i, j])
    for e in range(2):
        pr[:, lay['pscale'] + e * 4: lay['pscale'] + e * 4 + 4] = fm(inp['pool_scale'][e])
        for kk in range(3):
            b = lay['sconv'] + (e * 3 + kk) * 4
            pr[:, b:b + 4] = fm(inp['w_sconv'][e, kk])
    for i in range(4):
        for kk in range(3):
            b = lay['fconv'] + (i * 3 + kk) * 22
            pr[:, b:b + 22] = fm(inp['w_ffn_conv'][i, kk])
    for o in range(2):
        b = lay['qnorm'] + o * 3
        pr[:, b:b + 3] = fm(inp['q_norm'][o])
    return pr


def const_layout(cfg):
    lay = {}
    off = 0
    lay['ident'] = off; off += 128
    lay['ones'] = off; off += 128
    lay['tri'] = off; off += 128
    lay['msk'] = off; off += 16 * cfg.WS
    lay['nb16'] = off
    lay['invc'] = off; off += 4 * 16
    lay['ropefm'] = off; off += cfg.S + cfg.WS
    lay['ropetm'] = off; off += (cfg.S // 128) * 64
    lay['ropetms'] = off; off += 64
    lay['eps'] = off; off += 1
    lay['pidx'] = off; off += 1
    return lay, off


def build_consts(cfg):
    lay, n = const_layout(cfg)
    cs = np.zeros((128, n), np.float32)
    cs[:, lay['ident']:lay['ident'] + 128] = np.eye(128, dtype=np.float32)
    cs[:, lay['ones']:lay['ones'] + 128] = 1.0
    p = np.arange(128)
    cs[:, lay['tri']:lay['tri'] + 128] = (p[:, None] <= p[None, :]).astype(np.float32)
    WS = cfg.WS
    msk = np.zeros((128, 16, WS), np.float32)
    for kb in range(cfg.NB):
        for kj in range(cfg.DEC):
            for qj in range(kj, cfg.DEC):
                msk[kb * cfg.DEC + kj, :, kb * cfg.DEC + qj] = 1.0
    cs[:, lay['msk']:lay['msk'] + 16 * WS] = msk.reshape(128, 16 * WS)
    for m, w in enumerate(cfg.POOL_W):
        t = np.arange(16)
        cs[:, lay['invc'] + m * 16: lay['invc'] + m * 16 + 16] = (1.0 / np.minimum(t + 1, w)).astype(np.float32)[None, :]
    half = 32
    freqs = (np.float32(10000.0) ** (-np.arange(half, dtype=np.float32) / np.float32(half))).astype(np.float32)
    pos = np.concatenate([np.arange(cfg.S), cfg.PAST + (np.arange(WS) % cfg.DEC)]).astype(np.float32)
    ang = (pos[:, None] * freqs[None, :]).astype(np.float32)
    cos = np.cos(ang).astype(np.float32).T
    sin = np.sin(ang).astype(np.float32).T
    rf = np.concatenate([cos, cos, -sin, sin], axis=0)
    cs[:, lay['ropefm']:lay['ropefm'] + cfg.S + WS] = rf
    nt = cfg.S // 128
    posp = (np.arange(nt)[None, :] * 128 + p[:, None]).astype(np.float32)
    angp = (posp[:, :, None] * freqs[None, None, :]).astype(np.float32)
    tm = np.concatenate([np.cos(angp), np.sin(angp)], axis=2).astype(np.float32)
    cs[:, lay['ropetm']:lay['ropetm'] + nt * 64] = tm.reshape(128, nt * 64)
    poss = (cfg.PAST + (p % cfg.DEC)).astype(np.float32)
    angs = (poss[:, None] * freqs[None, :]).astype(np.float32)
    cs[:, lay['ropetms']:lay['ropetms'] + 64] = np.concatenate([np.cos(angs), np.sin(angs)], axis=1).astype(np.float32)
    cs[:, lay['eps']] = cfg.EPS
    cs[:, lay['pidx']] = np.arange(128, dtype=np.float32)
    return cs


def build_nc(cfg):
    nc = bass.Bass("TRN2", target_bir_lowering=False)
    D, KD, KF, W, S, NSEQ, NB, DEC, NPG = cfg.D, cfg.KD, cfg.KF, cfg.W, cfg.S, cfg.NSEQ, cfg.NB, cfg.DEC, cfg.NPG
    WS = cfg.WS
    HQ = 16 * DEC
    NTS = S // W
    NKT = S // 128
    ptab, wtot = piece_table()
    play, npr = param_layout()
    clay, ncs = const_layout(cfg)
    NBC = 3 * 128 + 16 * WS
    SCALE = float((128 + 64) ** -0.5)

    def din(name, shape, dt=F32):
        return nc.dram_tensor(name, list(shape), dt, kind="ExternalInput").ap()

    def dout(name, shape, dt=F32):
        return nc.dram_tensor(name, list(shape), dt, kind="ExternalOutput").ap()

    d_wst = din("wst", [128, wtot])
    d_pr = din("pr", [128, npr])
    d_kvg = din("kvg", [128, 512])
    d_cs = din("cs", [128, ncs])
    d_xp = din("xp", [NSEQ, 128, KD, S])
    d_xs = din("xs", [128, KD, WS])
    d_spool = din("spool", [2, 128, 4, NB, 15])
    d_ssconv = din("ssconv", [2, 128, 4, NB, 2])
    d_sffn = din("sffn", [4, 128, KF, NB, 2])
    d_cache = [din(f"cache{o}", [cfg.NPOOLPG * 128, 320]) for o in range(2)]
    d_ptb = din("ptb", [128, NB * NPG], I32)

    o_yp = dout("o_yp", [NSEQ, 128, KD, S])
    o_ys = dout("o_ys", [128, KD, WS])
    o_poolp = dout("o_poolp", [2, NSEQ, 128, 4, 15])
    o_pools = dout("o_pools", [2, 128, 4, NB, 15])
    o_sconvp = dout("o_sconvp", [2, NSEQ, 128, 4, 2])
    o_sconvs = dout("o_sconvs", [2, 128, 4, NB, 2])
    o_ckvp = dout("o_ckvp", [2, NSEQ, S, 256])
    o_krp = dout("o_krp", [2, NSEQ, S, 64])
    o_ckvs = dout("o_ckvs", [2, WS, 256])
    o_krs = dout("o_krs", [2, WS, 64])
    o_ffnp = dout("o_ffnp", [4, NSEQ, 128, KF, 2])
    o_ffns = dout("o_ffns", [4, 128, KF, NB, 2])

    with ExitStack() as st:
        S_ = Sched(nc, st)
        op = S_.op
        fence = {}
        cnt = {'ld': 0, 'st': 0}

        def sb(name, shape, dt):
            return st.enter_context(nc.sbuf_tensor(name, list(shape), dt))

        def bufs(name, n, dma=False):
            return [S_.buf(f"{name}{i}", dma) for i in range(n)]

        class Stage:
            def __init__(self):
                self.es = ExitStack()
                self.bl = []

            def sb(self, name, shape, dt, n=1, dma=False):
                cnt['nm'] = cnt.get('nm', 0) + 1
                name = f"{name}_{cnt['nm']}"
                t = self.es.enter_context(nc.sbuf_tensor(name, list(shape), dt))
                bl = []
                for i in range(n):
                    b = S_.buf(f"{name}{i}", False)
                    b.reads = list(fence.values())
                    bl.append(b)
                    self.bl.append(b)
                return t, bl

            def close(self):
                for b in self.bl:
                    for o_ in ([b.last_write] if b.last_write is not None else []) + list(b.reads):
                        key = ('d', id(o_.dsem)) if o_.is_dma else ('e', o_.eng)
                        if key not in fence or fence[key].idx < o_.idx:
                            fence[key] = o_
                self.es.close()

        ds_ld = [S_.new_dsem(f"ld{i}") for i in range(5)]
        ds_st = [S_.new_dsem(f"st{i}") for i in range(4)]


        def nxt_st():
            cnt['st'] += 1
            return ds_st[cnt['st'] % 4]

        NCF = ncs - NBC
        PR = sb("PR", [128, npr], F32); b_PR = S_.buf("PR", True)
        KVG = sb("KVG", [128, 512], F32); b_KVG = S_.buf("KVG", True)
        CS = sb("CS", [128, NCF], F32); b_CS = S_.buf("CS", True)
        CB = sb("CB", [128, NBC], BF16); b_CB = S_.buf("CB", True)
        op('sp', lambda e: e.dma_start(out=PR[:], in_=d_pr), writes=[b_PR], dma=b_PR.dsem)
        op('sp', lambda e: e.dma_start(out=KVG[:], in_=d_kvg), writes=[b_KVG], dma=b_KVG.dsem)
        op('sp', lambda e: e.dma_start(out=CS[:], in_=d_cs[:, NBC:ncs]), writes=[b_CS], dma=b_CS.dsem)
        op('pool', lambda e: e.dma_start(out=CB[:], in_=d_cs[:, 0:NBC]), writes=[b_CB], dma=b_CB.dsem)
        IDB = CB[:, 0:128]
        ONESB = CB[:, 128:256]
        TRIB = CB[:, 256:384]
        MSKB = CB[:, 384:384 + 16 * WS]

        def csc(name, a=0, n=1):
            c = clay[name] - NBC + a
            return CS[:, c:c + n]

        EPS = csc('eps')

        def prc(name, idx):
            c = play[name] + idx
            return PR[:, c:c + 1]

        SLOTC = 4096
        slots = [sb(f"slot{i}", [128, SLOTC], BF16) for i in range(cfg.NSLOT)]
        b_slots = bufs("slot", cfg.NSLOT, True)
        ring = {'n': 0}

        def load_piece(name):
            off, cols = ptab[name]
            i = ring['n'] % cfg.NSLOT
            ring['n'] += 1
            sl, b = slots[i], b_slots[i]
            op('pool', lambda e, sl=sl, off=off, cols=cols: e.dma_start(out=sl[:, 0:cols], in_=d_wst[:, off:off + cols]),
               writes=[b], dma=b.dsem)
            return sl, b

        PS = [st.enter_context(nc.psum_tensor(f"ps{i}", [128, 512], F32)) for i in range(7)]
        b_PS = bufs("ps", 7)
        PT = st.enter_context(nc.psum_tensor("pst", [128, 1024], BF16))
        b_PT = S_.buf("pst")
        rot = {'g': 0, 'n': 0}

        def ps_gen():
            i = rot['g'] % 6
            rot['g'] += 1
            return PS[i], b_PS[i]

        X = sb("X", [128, KD, W], F32); b_X = bufs("X", KD, True)
        XN = sb("XN", [128, KD, W], BF16); b_XN = bufs("XN", KD)
        YO = sb("YO", [128, KD, W], F32); b_YO = bufs("YO", KD)
        SQ = sb("SQ", [128, 2, W], BF16); b_SQ = bufs("SQ", 2)
        SD = sb("SD", [128, W], F32); b_SD = S_.buf("SD")
        TMP = sb("TMP", [128, 2, W], F32); b_TMP = bufs("TMP", 2)
        UH = [sb(f"UH{e}", [128, 4, 16], F32) for e in range(2)]; b_UH = bufs("UH", 2)
        VH = [sb(f"VH{e}", [128, 4, 2], F32) for e in range(2)]; b_VH = bufs("VH", 2)
        GH = [sb(f"GH{i}", [128, KF, 2], F32) for i in range(4)]
        b_GH = [S_.buf(f"GH{i}", True) for i in range(4)]
        b_GHq = [bufs(f"GHq{i}_", KF) for i in range(4)]
        KT = [sb(f"KT{o}", [128, 3, S], BF16) for o in range(2)]
        b_KT = [bufs(f"KT{o}_", NKT) for o in range(2)]
        VV = [sb(f"VV{o}", [128, NKT, 256], BF16) for o in range(2)]
        b_VV = [bufs(f"VV{o}_", NKT) for o in range(2)]
        SS = sb("SS", [128, 8], F32); b_SS = S_.buf("SS")
        for o_z in range(2):
            op('dve', lambda e, o_z=o_z: e.memset(KT[o_z][64:128, 2, :], 0.0), writes=b_KT[o_z])
        ds_idx = S_.new_dsem("idx")

        def matmul_group(ps_ap, pairs, reads_list, b_out):
            n = len(pairs)
            for i, (l, r) in enumerate(pairs):
                op('pe', lambda e, l=l, r=r, i=i: e.matmul(ps_ap, lhsT=l, rhs=r, start=(i == 0), stop=(i == n - 1)),
                   reads=list(reads_list[i]), writes=[b_out])

        def rms_stats(src, b_src, nk, wd, dim):
            rot['n'] += 1
            ps, bp = PS[6], b_PS[6]
            for k in range(nk):
                if k % 2 == 0:
                    op('act', lambda e, k=k: e.activation(out=SQ[:, k % 2, 0:wd], in_=src[:, k, 0:wd], func=AF.Square),
                       reads=[b_src[k]], writes=[b_SQ[k % 2]])
                else:
                    op('dve', lambda e, k=k: e.tensor_tensor(out=SQ[:, k % 2, 0:wd], in0=src[:, k, 0:wd], in1=src[:, k, 0:wd], op=ALU.mult),
                       reads=[b_src[k]], writes=[b_SQ[k % 2]])
                op('pe', lambda e, k=k: e.matmul(ps[:, 0:wd], lhsT=ONESB, rhs=SQ[:, k % 2, 0:wd], start=(k == 0), stop=(k == nk - 1)),
                   reads=[b_SQ[k % 2], b_CB], writes=[bp])
            op('act', lambda e: e.activation(out=SD[:, 0:wd], in_=ps[:, 0:wd], func=AF.Sqrt, bias=EPS, scale=1.0 / dim),
               reads=[bp, b_CS], writes=[b_SD])
            op('dve', lambda e: e.reciprocal(out=SD[:, 0:wd], in_=SD[:, 0:wd]), reads=[b_SD], writes=[b_SD])

        def prenorm(gbase, wd):
            rms_stats(X, b_X, KD, wd, D)
            for k in range(KD):
                op('dve', lambda e, k=k: e.scalar_tensor_tensor(out=XN[:, k, 0:wd], in0=X[:, k, 0:wd], scalar=prc('norm', gbase + k),
                                                                 in1=SD[:, 0:wd], op0=ALU.mult, op1=ALU.mult),
                   reads=[b_X[k], b_SD, b_PR], writes=[b_XN[k]])

        def postnorm(gbase, wd):
            rms_stats(YO, b_YO, KD, wd, D)
            for k in range(KD):
                t = k % 2
                op('dve', lambda e, k=k, t=t: e.scalar_tensor_tensor(out=TMP[:, t, 0:wd], in0=YO[:, k, 0:wd], scalar=prc('norm', gbase + k),
                                                                      in1=SD[:, 0:wd], op0=ALU.mult, op1=ALU.mult),
                   reads=[b_YO[k], b_SD, b_PR], writes=[b_TMP[t]])
                op('dve', lambda e, k=k, t=t: e.tensor_tensor(out=X[:, k, 0:wd], in0=X[:, k, 0:wd], in1=TMP[:, t, 0:wd], op=ALU.add),
                   reads=[b_X[k], b_TMP[t]], writes=[b_X[k]])

        def proj_fm(sl, b_sl, ncols_piece, col0, src, b_src, nk, wd, base=0):
            ps, bp = ps_gen()
            matmul_group(ps[:, 0:wd],
                         [(sl[:, base + k * ncols_piece + col0: base + k * ncols_piece + col0 + 128], src[:, k, 0:wd]) for k in range(nk)],
                         [[b_sl, b_src[k]] for k in range(nk)], bp)
            return ps, bp

        def act_copy(out, in_, reads, writes, scale=None):
            if scale is None:
                op('act', lambda e: e.activation(out=out, in_=in_, func=AF.Copy), reads=reads, writes=writes)
            else:
                op('act', lambda e: e.activation(out=out, in_=in_, func=AF.Copy, scale=scale), reads=reads, writes=writes)

        def ffn(i, kind, wd, seq, last):
            sg = Stage()
            G, b_G = sg.sb("G", [128, 3, 2 + W], F32, 3)
            GA, b_GA = sg.sb("GA", [128, 3, W], F32, 3)
            ACTB, b_ACTB = sg.sb("ACTB", [128, KF, W], BF16, KF)
            if kind == 's':
                SF, (b_SF,) = sg.sb("SF", [128, KF, NB, 2], F32)
                GSO, (b_GSO,) = sg.sb("GSO", [128, KF, NB, 2], F32)
                op('sp', lambda e: e.dma_start(out=SF[:, :, :, :], in_=d_sffn[i]), writes=[b_SF], dma=ds_ld[4])
            prenorm((i * 4 + 2) * 8, wd)
            nb_, L = (1, 2 + wd) if kind == 'p' else (NB, 6)
            nw = L - 2
            pendB = []

            def stage_b(q, gi, psu, bpu):
                bg, bga = b_G[gi], b_GA[gi]
                gv = G[:, gi, 0:nb_ * L].rearrange("p (b l) -> p b l", l=L)
                gaf = GA[:, gi, 0:wd]
                op('act', lambda e, gaf=gaf: e.activation(out=gaf, in_=gaf, func=AF.Silu), reads=[bga], writes=[bga])
                op('dve', lambda e, gaf=gaf, psu=psu, q=q: e.tensor_tensor(out=ACTB[:, q, 0:wd], in0=psu[:, 0:wd], in1=gaf, op=ALU.mult),
                   reads=[bpu, bga], writes=[b_ACTB[q]])
                if kind == 'p':
                    op('dve', lambda e, gv=gv, q=q: e.tensor_copy(out=GH[i][:, q, :], in_=gv[:, 0, wd:wd + 2]), reads=[bg], writes=[b_GHq[i][q]])
                else:
                    op('dve', lambda e, gv=gv, q=q: e.tensor_copy(out=GSO[:, q, :, :], in_=gv[:, :, 4:6]), reads=[bg], writes=[b_GSO])

            for j in range(11):
                sl, bsl = load_piece(f'F_up{i}_{j}')
                for cc in range(2):
                    q = 2 * j + cc
                    gi = q % 3
                    bg, bga = b_G[gi], b_GA[gi]
                    psg, bpg = proj_fm(sl, bsl, 256, cc * 128, XN, b_XN, KD, wd)
                    psu, bpu = proj_fm(sl, bsl, 256, cc * 128, XN, b_XN, KD, wd, base=2048)
                    gv = G[:, gi, 0:nb_ * L].rearrange("p (b l) -> p b l", l=L)
                    gav = GA[:, gi, 0:nb_ * nw].rearrange("p (b l) -> p b l", l=nw)
                    if kind == 'p':
                        act_copy(gv[:, 0, 0:2], GH[i][:, q, :], [b_GHq[i][q]], [bg])
                    else:
                        act_copy(gv[:, :, 0:2], SF[:, q, :, :], [b_SF], [bg])
                    act_copy(gv[:, :, 2:L], psg[:, 0:wd].rearrange("p (b l) -> p b l", l=nw), [bpg], [bg])
                    w0, w1, w2 = (prc('fconv', (i * 3 + kk) * 22 + q) for kk in range(3))
                    act_copy(gav, gv[:, :, 0:nw], [bg, b_PR], [bga], scale=w0)
                    op('dve', lambda e, gv=gv, gav=gav, w1=w1: e.scalar_tensor_tensor(out=gav, in0=gv[:, :, 1:1 + nw], scalar=w1, in1=gav, op0=ALU.mult, op1=ALU.add),
                       reads=[bg, bga, b_PR], writes=[bga])
                    op('dve', lambda e, gv=gv, gav=gav, w2=w2: e.scalar_tensor_tensor(out=gav, in0=gv[:, :, 2:2 + nw], scalar=w2, in1=gav, op0=ALU.mult, op1=ALU.add),
                       reads=[bg, bga, b_PR], writes=[bga])
                    if pendB:
                        stage_b(*pendB.pop(0))
                    pendB.append((q, gi, psu, bpu))
            while pendB:
                stage_b(*pendB.pop(0))
            for c in range(8):
                sl, bsl = load_piece(f'F_dn{i}_{c}')
                ps, bp = proj_fm(sl, bsl, 128, 0, ACTB, b_ACTB, KF, wd)
                act_copy(YO[:, c, 0:wd], ps[:, 0:wd], [bp], [b_YO[c]])
            postnorm((i * 4 + 3) * 8, wd)
            if kind == 'p' and last:
                op('sp', lambda e: e.dma_start(out=o_ffnp[i, seq], in_=GH[i][:, :, :]), reads=[b_GH[i]] + b_GHq[i], dma=b_GH[i].dsem, store=True)
            if kind == 's':
                op('sp', lambda e: e.dma_start(out=o_ffns[i], in_=GSO[:, :, :, :]), reads=[b_GSO], dma=nxt_st(), store=True)
            sg.close()

        def even(e_, kind, wd, first, last, i, seq):
            sg = Stage()
            if kind == 'p':
                nb_, HU, LU, LV = 1, 16, 16 + wd, 2 + wd
            else:
                nb_, HU, LU, LV = NB, 15, 19, 6
            U, bu = sg.sb("U", [128, 4, nb_ * LU], F32, 4)
            V, bv = sg.sb("V", [128, 4, nb_ * LV], F32, 4)
            T1, (b_T1,) = sg.sb("T1", [128, nb_ * LU], F32)
            T2, (b_T2,) = sg.sb("T2", [128, nb_ * LU], F32)
            DB, b_DB = sg.sb("DB", [128, 4, W], BF16, 4)
            YC, b_YC = sg.sb("YC", [128, 8, W], BF16, 8)
            HX, b_HX = YO, b_YO
            nw = LU - HU
            uv = lambda m: U[:, m, :].rearrange("p (b l) -> p b l", l=LU)
            vv = lambda m: V[:, m, :].rearrange("p (b l) -> p b l", l=LV)
            t1v = T1[:, :].rearrange("p (b l) -> p b l", l=LU)
            t2v = T2[:, :].rearrange("p (b l) -> p b l", l=LU)
            v3 = lambda ap: ap.rearrange("p (b l) -> p b l", l=nw)
            if kind == 'p':
                for m in range(4):
                    op('dve', lambda e, m=m: e.tensor_copy(out=U[:, m, 0:16], in_=UH[e_][:, m, :]), reads=[b_UH[e_]], writes=[bu[m]])
                    op('dve', lambda e, m=m: e.tensor_copy(out=V[:, m, 0:2], in_=VH[e_][:, m, :]), reads=[b_VH[e_]], writes=[bv[m]])
            else:
                for m in range(4):
                    dl = ds_ld[m]
                    op('sp', lambda e, m=m: e.dma_start(out=uv(m)[:, :, 0:15], in_=d_spool[e_, :, m]), writes=[bu[m]], dma=dl)
                    op('sp', lambda e, m=m: e.dma_start(out=vv(m)[:, :, 0:2], in_=d_ssconv[e_, :, m]), writes=[bv[m]], dma=dl)
            prenorm((i * 4 + 0) * 8, wd)
            sl, bsl = load_piece(f'E_u{e_}')
            for m in range(4):
                ps, bp = proj_fm(sl, bsl, 512, m * 128, XN, b_XN, KD, wd)
                act_copy(uv(m)[:, :, HU:LU], v3(ps[:, 0:wd]), [bp], [bu[m]])
            for m in range(4):
                um = uv(m)
                wpool = cfg.POOL_W[m]
                cur, bcur = um, bu[m]
                sh = 1
                tgl = 0
                for step in range(m + 1):
                    dst, bdst = (t1v, b_T1) if tgl == 0 else (t2v, b_T2)
                    a = 2 * sh - 1
                    op('dve', lambda e, cur=cur, dst=dst, a=a, sh=sh: e.tensor_tensor(out=dst[:, :, a:LU], in0=cur[:, :, a:LU], in1=cur[:, :, a - sh:LU - sh], op=ALU.add),
                       reads=[bcur], writes=[bdst])
                    cur, bcur = dst, bdst
                    sh *= 2
                    tgl ^= 1
                op('dve', lambda e, cur=cur, um=um, m=m, wpool=wpool: e.scalar_tensor_tensor(out=v3(DB[:, m, 0:wd]), in0=cur[:, :, HU:LU], scalar=1.0 / wpool, in1=um[:, :, HU:LU],
                                                                            op0=ALU.mult, op1=ALU.subtract),
                   reads=[bcur, bu[m]], writes=[b_DB[m]])
                if kind == 'p' and first:
                    ic = csc('invc', m * 16, 16)
                    op('dve', lambda e, cur=cur, ic=ic: e.tensor_tensor(out=TMP[:, 0, 0:16], in0=cur[:, 0, HU:HU + 16], in1=ic, op=ALU.mult),
                       reads=[bcur, b_CS], writes=[b_TMP[0]])
                    op('dve', lambda e, um=um, m=m: e.tensor_tensor(out=DB[:, m, 0:16], in0=TMP[:, 0, 0:16], in1=um[:, 0, HU:HU + 16], op=ALU.subtract),
                       reads=[b_TMP[0], bu[m]], writes=[b_DB[m]])
            sl, bsl = load_piece(f'E_hx{e_}')
            for m in range(4):
                ps, bp = proj_fm(sl, bsl, 512, m * 128, XN, b_XN, KD, wd)
                act_copy(HX[:, m, 0:wd], ps[:, 0:wd], [bp], [b_HX[m]])
            sl, bsl = load_piece(f'E_gc{e_}')
            for m in range(4):
                ps, bp = proj_fm(sl, bsl, 512, m * 128, XN, b_XN, KD, wd)
                vm = vv(m)
                op('dve', lambda e, ps=ps, m=m, vm=vm: e.tensor_tensor(out=vm[:, :, 2:LV], in0=v3(ps[:, 0:wd]), in1=v3(HX[:, m, 0:wd]), op=ALU.mult),
                   reads=[bp, b_HX[m]], writes=[bv[m]])
                w0, w1, w2 = (prc('sconv', (e_ * 3 + kk) * 4 + m) for kk in range(3))
                accv = v3(YO[:, 4 + m, 0:wd])
                bacc = b_YO[4 + m]
                act_copy(accv, vm[:, :, 0:nw], [bv[m], b_PR], [bacc], scale=w0)
                op('dve', lambda e, vm=vm, accv=accv, w1=w1: e.scalar_tensor_tensor(out=accv, in0=vm[:, :, 1:1 + nw], scalar=w1, in1=accv, op0=ALU.mult, op1=ALU.add),
                   reads=[bv[m], bacc, b_PR], writes=[bacc])
                op('dve', lambda e, vm=vm, accv=accv, w2=w2: e.scalar_tensor_tensor(out=accv, in0=vm[:, :, 2:2 + nw], scalar=w2, in1=accv, op0=ALU.mult, op1=ALU.add),
                   reads=[bv[m], bacc, b_PR], writes=[bacc])
            sl, bsl = load_piece(f'E_gb{e_}')
            for m in range(4):
                ps, bp = proj_fm(sl, bsl, 512, m * 128, XN, b_XN, KD, wd)
                op('dve', lambda e, ps=ps, m=m: e.tensor_tensor(out=YC[:, 4 + m, 0:wd], in0=ps[:, 0:wd], in1=YO[:, 4 + m, 0:wd], op=ALU.mult),
                   reads=[bp, b_YO[4 + m]], writes=[b_YC[4 + m]])
            sl, bsl = load_piece(f'E_map{e_}')
            for m in range(4):
                ps, bp = ps_gen()
                matmul_group(ps[:, 0:wd], [(sl[:, m * 128:(m + 1) * 128], DB[:, m, 0:wd])], [[bsl, b_DB[m]]], bp)
                act_copy(YC[:, m, 0:wd], ps[:, 0:wd], [bp, b_PR], [b_YC[m]], scale=prc('pscale', e_ * 4 + m))
            for hh in range(2):
                sl, bsl = load_piece(f'E_out{e_}_{hh}')
                for cc in range(4):
                    c = hh * 4 + cc
                    ps, bp = proj_fm(sl, bsl, 512, cc * 128, YC, b_YC, 8, wd)
                    act_copy(YO[:, c, 0:wd], ps[:, 0:wd], [bp], [b_YO[c]])
            postnorm((i * 4 + 1) * 8, wd)
            if kind == 'p':
                if last:
                    ds = nxt_st()
                    op('sp', lambda e: e.dma_start(out=o_poolp[e_, seq], in_=U[:, :, wd + 1:wd + 16]), reads=bu, dma=ds, store=True)
                    op('sp', lambda e: e.dma_start(out=o_sconvp[e_, seq], in_=V[:, :, wd:wd + 2]), reads=bv, dma=ds, store=True)
                else:
                    op('dve', lambda e: e.tensor_copy(out=UH[e_][:, :, :], in_=U[:, :, wd:wd + 16]), reads=bu, writes=[b_UH[e_]])
                    op('dve', lambda e: e.tensor_copy(out=VH[e_][:, :, :], in_=V[:, :, wd:wd + 2]), reads=bv, writes=[b_VH[e_]])
            else:
                ds = nxt_st()
                for m in range(4):
                    op('sp', lambda e, m=m: e.dma_start(out=o_pools[e_, :, m], in_=uv(m)[:, :, 4:19]), reads=[bu[m]], dma=ds, store=True)
                    op('sp', lambda e, m=m: e.dma_start(out=o_sconvs[e_, :, m], in_=vv(m)[:, :, 4:6]), reads=[bv[m]], dma=ds, store=True)
            sg.close()

        def mla(o_, kind, wd, i, seq, tq):
            sg = Stage()
            CQ, b_CQ = YO, b_YO
            RT, b_RT = YO[:, 4:6, :], b_YO[4:6]
            RS, b_RS = YO[:, 6, :], b_YO[6]
            WA = W if kind == 'p' else WS
            CQN, b_CQN = sg.sb("CQN", [128, 3, WA], BF16, 3)
            CKV, b_CKV = sg.sb("CKV", [128, 2, 320], F32, 2)
            KB, b_KB = sg.sb("KB", [128, 2, 320], BF16, 2)
            NHS = 8 if kind == 'p' else 2
            QN, b_QN = sg.sb("QN", [128, NHS, WA], BF16, NHS)
            OH, b_OH = sg.sb("OH", [128, 16, WA], BF16, 16)
            if kind == 'p':
                QR, b_QR = sg.sb("QR", [128, NHS, W], BF16, NHS)
                op('dve', lambda e: e.memset(QR[64:128, :, :], 0.0), writes=b_QR)
                QL, b_QL = sg.sb("QL", [128, 2, 2, W], BF16, 2)
                PB, b_PB = sg.sb("PB", [128, 4, W], BF16, 4)
                OL, b_OL = sg.sb("OL", [128, 2, 2, W], BF16, 2)
            else:
                QR, b_QRl = sg.sb("QRS", [128, 16, WS], BF16, 1)
                b_QR = b_QRl * 16
                op('dve', lambda e: e.memset(QR[64:128, :, :], 0.0), writes=[b_QR[0]])
                QLS, (b_QLS,) = sg.sb("QLS", [128, 2, 16, WS], BF16)
                KTN, (b_KTN,) = sg.sb("KTN", [128, 3, WS], BF16)
                PN, (b_PN,) = sg.sb("PN", [128, 16 * WS], BF16)
                ON, (b_ON,) = sg.sb("ON", [128, 2, 16 * WS], F32)
                SNW, (b_SNW,) = sg.sb("SNW", [128, 16 * WS], F32)
                OLS, (b_OLS,) = sg.sb("OLS", [128, 2, 16, WS], BF16)
                NGB = 8
                KBP, b_KBP = sg.sb("KBP", [128, NGB, 384], BF16, NGB)
                op('dve', lambda e: e.memset(KBP[:, :, 320:384], 0.0), writes=b_KBP)
                ds_g = [S_.new_dsem(f"g{o_}_{g}") for g in range(NGB)]
                KTP, b_KTP = sg.sb("KTP", [128, 2, 384], BF16, 2)
                PP, b_PP = sg.sb("PP", [128, 2, HQ], BF16, 2)
                TS_, (b_TS,) = sg.sb("TS", [128, 3, HQ], F32)
                IDX, (b_IDX,) = sg.sb("IDX", [128, NB * NPG], I32)
                PTF, (b_PTF,) = sg.sb("PTF", [128, NB * NPG], F32)
                op('sp', lambda e: e.dma_start(out=IDX[:], in_=d_ptb), writes=[b_IDX], dma=ds_idx)
                op('dve', lambda e: e.tensor_copy(out=PTF[:], in_=IDX[:]), reads=[b_IDX], writes=[b_PTF])
                op('dve', lambda e: e.tensor_scalar(out=PTF[:], in0=PTF[:], scalar1=128.0, scalar2=csc('pidx'), op0=ALU.mult, op1=ALU.add),
                   reads=[b_PTF, b_CS], writes=[b_PTF])
                op('dve', lambda e: e.tensor_copy(out=IDX[:], in_=PTF[:]), reads=[b_PTF], writes=[b_IDX])
            prenorm((i * 4 + 0) * 8, wd)
            sl, bsl = load_piece(f'O_q{o_}')
            for k3 in range(3):
                ps, bp = proj_fm(sl, bsl, 384, k3 * 128, XN, b_XN, KD, wd)
                act_copy(CQ[:, k3, 0:wd], ps[:, 0:wd], [bp], [b_CQ[k3]])
            rms_stats(CQ, b_CQ, 3, wd, 384)
            for k3 in range(3):
                op('dve', lambda e, k3=k3: e.scalar_tensor_tensor(out=CQN[:, k3, 0:wd], in0=CQ[:, k3, 0:wd], scalar=prc('qnorm', o_ * 3 + k3),
                                                                   in1=SD[:, 0:wd], op0=ALU.mult, op1=ALU.mult),
                   reads=[b_CQ[k3], b_SD, b_PR], writes=[b_CQN[k3]])
            sl, bsl = load_piece(f'O_kv{o_}')
            nsub = (wd + 127) // 128
            for ts in range(nsub):
                nt = min(128, wd - ts * 128)
                ci = ts % 2
                ckv, bckv, kb, bkb = CKV[:, ci, :], b_CKV[ci], KB[:, ci, :], b_KB[ci]
                ps, bp = ps_gen()
                matmul_group(ps[0:nt, 0:320], [(XN[:, k, ts * 128: ts * 128 + nt], sl[:, k * 320:(k + 1) * 320]) for k in range(KD)],
                             [[bsl, b_XN[k]] for k in range(KD)], bp)
                op('dve', lambda e, nt=nt: e.memset(SS[0:nt, 0:1], 0.0), writes=[b_SS])
                op('act', lambda e, ps=ps, nt=nt, ckv=ckv: e.activation(out=ckv[0:nt, 0:256], in_=ps[0:nt, 0:256], func=AF.Square, accum_out=SS[0:nt, 0:1]),
                   reads=[bp, b_SS], writes=[bckv, b_SS])
                op('act', lambda e, nt=nt: e.activation(out=SS[0:nt, 1:2], in_=SS[0:nt, 0:1], func=AF.Sqrt, bias=EPS[0:nt, :], scale=1.0 / 256),
                   reads=[b_SS, b_CS], writes=[b_SS])
                op('dve', lambda e, nt=nt: e.reciprocal(out=SS[0:nt, 2:3], in_=SS[0:nt, 1:2]), reads=[b_SS], writes=[b_SS])
                op('dve', lambda e, ps=ps, nt=nt, ckv=ckv: e.scalar_tensor_tensor(out=ckv[0:nt, 0:256], in0=ps[0:nt, 0:256], scalar=SS[0:nt, 2:3],
                                                                                   in1=KVG[0:nt, o_ * 256:(o_ + 1) * 256], op0=ALU.mult, op1=ALU.mult),
                   reads=[bp, b_SS, b_KVG], writes=[bckv])
                if kind == 'p':
                    gt = tq * (wd // 128) + ts
                    tb = csc('ropetm', gt * 64, 64)
                else:
                    tb = csc('ropetms', 0, 64)
                cos_, sin_ = tb[0:nt, 0:32], tb[0:nt, 32:64]
                x1, x2 = ps[0:nt, 256:288], ps[0:nt, 288:320]
                o1, o2 = ckv[0:nt, 256:288], ckv[0:nt, 288:320]
                t1_, t2_ = TMP[0:nt, 0, 0:32], TMP[0:nt, 0, 32:64]
                op('dve', lambda e, x1=x1, cos_=cos_, o1=o1: e.tensor_tensor(out=o1, in0=x1, in1=cos_, op=ALU.mult), reads=[bp, b_CS], writes=[bckv])
                op('dve', lambda e, x2=x2, sin_=sin_, t1_=t1_: e.tensor_tensor(out=t1_, in0=x2, in1=sin_, op=ALU.mult), reads=[bp, b_CS], writes=[b_TMP[0]])
                op('dve', lambda e, o1=o1, t1_=t1_: e.tensor_tensor(out=o1, in0=o1, in1=t1_, op=ALU.subtract), reads=[bckv, b_TMP[0]], writes=[bckv])
                op('dve', lambda e, x2=x2, cos_=cos_, o2=o2: e.tensor_tensor(out=o2, in0=x2, in1=cos_, op=ALU.mult), reads=[bp, b_CS], writes=[bckv])
                op('dve', lambda e, x1=x1, sin_=sin_, t2_=t2_: e.tensor_tensor(out=t2_, in0=x1, in1=sin_, op=ALU.mult), reads=[bp, b_CS], writes=[b_TMP[0]])
                op('dve', lambda e, o2=o2, t2_=t2_: e.tensor_tensor(out=o2, in0=o2, in1=t2_, op=ALU.add), reads=[bckv, b_TMP[0]], writes=[bckv])
                ds = nxt_st()
                if kind == 'p':
                    r0 = tq * wd + ts * 128
                    op('sp', lambda e, ckv=ckv, r0=r0, nt=nt: e.dma_start(out=o_ckvp[o_, seq, r0:r0 + nt, :], in_=ckv[0:nt, 0:256]), reads=[bckv], dma=ds, store=True)
                    op('sp', lambda e, ckv=ckv, r0=r0, nt=nt: e.dma_start(out=o_krp[o_, seq, r0:r0 + nt, :], in_=ckv[0:nt, 256:320]), reads=[bckv], dma=ds, store=True)
                else:
                    op('sp', lambda e, ckv=ckv, nt=nt: e.dma_start(out=o_ckvs[o_, 0:nt, :], in_=ckv[0:nt, 0:256]), reads=[bckv], dma=ds, store=True)
                    op('sp', lambda e, ckv=ckv, nt=nt: e.dma_start(out=o_krs[o_, 0:nt, :], in_=ckv[0:nt, 256:320]), reads=[bckv], dma=ds, store=True)
                act_copy(kb[0:nt, :], ckv[0:nt, :], [bckv], [bkb])
                for c in range(3):
                    wc = 128 if c < 2 else 64
                    op('pe', lambda e, c=c, wc=wc, kb=kb, nt=nt: e.transpose(out=PT[0:wc, c * 128: c * 128 + nt], in_=kb[0:nt, c * 128: c * 128 + wc], identity=IDB[0:nt, 0:nt]),
                       reads=[bkb, b_CB], writes=[b_PT])
                if kind == 'p':
                    gt = tq * (wd // 128) + ts
                    act_copy(KT[o_][:, 0:2, gt * 128:(gt + 1) * 128], PT[:, 0:256].rearrange("p (c t) -> p c t", t=128), [b_PT], [b_KT[o_][gt]])
                    act_copy(KT[o_][0:64, 2, gt * 128:(gt + 1) * 128], PT[0:64, 256:384], [b_PT], [b_KT[o_][gt]])
                    op('dve', lambda e, gt=gt, kb=kb: e.tensor_copy(out=VV[o_][:, gt, :], in_=kb[:, 0:256]), reads=[bkb], writes=[b_VV[o_][gt]])
                else:
                    for c in range(3):
                        wc = 128 if c < 2 else 64
                        act_copy(KTN[0:wc, c, 0:nt], PT[0:wc, c * 128:c * 128 + nt], [b_PT], [b_KTN])
            sluk, bsluk = load_piece(f'O_uk{o_}')
            sluv, bsluv = load_piece(f'O_uv{o_}')
            rb = (tq * wd) if kind == 'p' else S
            ropef = csc('ropefm', rb, wd)
            rr = {'n': 0}

            def bank56():
                rr['n'] += 1
                i_ = 5 + (rr['n'] % 2)
                return PS[i_], b_PS[i_]

            def q_head(sl, bsl, hh, h):
                hs_ = h % NHS
                ps, bp = bank56()
                matmul_group(ps[:, 0:wd], [(sl[:, k * 512 + hh * 128: k * 512 + hh * 128 + 128], CQN[:, k, 0:wd]) for k in range(3)],
                             [[bsl, b_CQN[k]] for k in range(3)], bp)
                act_copy(QN[:, hs_, 0:wd], ps[:, 0:wd], [bp], [b_QN[hs_]])
                ps2, bp2 = bank56()
                matmul_group(ps2[:, 0:wd], [(sl[:, 1536 + k * 512 + hh * 128: 1536 + k * 512 + hh * 128 + 128], CQN[:, k, 0:wd]) for k in range(3)],
                             [[bsl, b_CQN[k]] for k in range(3)], bp2)
                qro = QR[0:64, hs_, 0:wd] if kind == 'p' else QR[0:64, h, 0:wd]
                bqr = b_QR[hs_] if kind == 'p' else b_QR[0]
                op('dve', lambda e, ps2=ps2: e.tensor_tensor(out=RT[0:64, 0, 0:wd], in0=ps2[0:64, 0:wd], in1=ropef[0:64, :], op=ALU.mult),
                   reads=[bp2, b_CS], writes=[b_RT[0]])
                op('dve', lambda e, ps2=ps2: e.tensor_tensor(out=RT[0:64, 1, 0:wd], in0=ps2[64:128, 0:wd], in1=ropef[64:128, :], op=ALU.mult),
                   reads=[bp2, b_CS], writes=[b_RT[1]])
                op('dve', lambda e, qro=qro: e.tensor_tensor(out=qro, in0=RT[0:64, 0, 0:wd], in1=RT[0:64, 1, 0:wd], op=ALU.add),
                   reads=[b_RT[0], b_RT[1]], writes=[bqr])

            def q_heads(g):
                sl, bsl = load_piece(f'O_uq{o_}_{g}')
                for hh in range(4):
                    q_head(sl, bsl, hh, g * 4 + hh)
                    if kind == 's':
                        h = g * 4 + hh
                        for rc in range(2):
                            ps, bp = bank56()
                            matmul_group(ps[:, 0:wd], [(sluk[:, h * 256 + rc * 128: h * 256 + rc * 128 + 128], QN[:, h % NHS, 0:wd])], [[bsluk, b_QN[h % NHS]]], bp)
                            act_copy(QLS[:, rc, h, :], ps[:, 0:wd], [bp], [b_QLS])

            if kind == 'p':
                nkt = (tq + 1) * (wd // 128)

                def emit_qlat(h):
                    hs_ = h % NHS
                    ql, bql = QL[:, h % 2], b_QL[h % 2]
                    for rc in range(2):
                        ps, bp = PS[5 + rc], b_PS[5 + rc]
                        matmul_group(ps[:, 0:wd], [(sluk[:, h * 256 + rc * 128: h * 256 + rc * 128 + 128], QN[:, hs_, 0:wd])], [[bsluk, b_QN[hs_]]], bp)
                        act_copy(ql[:, rc, 0:wd], ps[:, 0:wd], [bp], [bql])

                def emit_S(h, kt, it):
                    hs_ = h % NHS
                    ql, bql = QL[:, h % 2], b_QL[h % 2]
                    jd = kt - tq * (wd // 128)
                    q0 = 0 if jd < 0 else jd * 128
                    pss, bps = PS[it % 2], b_PS[it % 2]
                    pb, bpb = PB[:, it % 4, :], b_PB[it % 4]
                    ks = slice(kt * 128, (kt + 1) * 128)
                    matmul_group(pss[:, q0:wd], [(KT[o_][:, 0, ks], ql[:, 0, q0:wd]), (KT[o_][:, 1, ks], ql[:, 1, q0:wd]),
                                                 (KT[o_][:, 2, ks], QR[:, hs_, q0:wd])],
                                 [[b_KT[o_][kt], bql], [b_KT[o_][kt], bql], [b_KT[o_][kt], b_QR[hs_]]], bps)
                    op('act', lambda e, pss=pss, pb=pb, q0=q0: e.activation(out=pb[:, q0:wd], in_=pss[:, q0:wd], func=AF.Exp, scale=SCALE), reads=[bps], writes=[bpb])
                    if jd >= 0:
                        op('dve', lambda e, pb=pb, q0=q0: e.tensor_tensor(out=pb[:, q0:q0 + 128], in0=pb[:, q0:q0 + 128], in1=TRIB, op=ALU.mult),
                           reads=[bpb, b_CB], writes=[bpb])
                    return (h, kt, pb, bpb, q0)

                def emit_V(info):
                    h, kt, pb, bpb, q0 = info
                    first, lastk = (kt == 0), (kt == nkt - 1)
                    for rc in range(2):
                        op('pe', lambda e, rc=rc, kt=kt, pb=pb, q0=q0, first=first, lastk=lastk: e.matmul(PS[2 + rc][:, q0:wd], lhsT=VV[o_][:, kt, rc * 128:(rc + 1) * 128], rhs=pb[:, q0:wd],
                                                                                                        start=first, stop=lastk),
                           reads=[b_VV[o_][kt], bpb], writes=[b_PS[2 + rc]])
                    op('pe', lambda e, pb=pb, q0=q0, first=first, lastk=lastk: e.matmul(PS[4][:, q0:wd], lhsT=ONESB, rhs=pb[:, q0:wd], start=first, stop=lastk),
                       reads=[b_CB, bpb], writes=[b_PS[4]])

                def emit_fin(h):
                    ol, bol = OL[:, h % 2], b_OL[h % 2]
                    op('dve', lambda e: e.reciprocal(out=RS[:, 0:wd], in_=PS[4][:, 0:wd]), reads=[b_PS[4]], writes=[b_RS])
                    for rc in range(2):
                        op('dve', lambda e, rc=rc, ol=ol: e.tensor_tensor(out=ol[:, rc, 0:wd], in0=PS[2 + rc][:, 0:wd], in1=RS[:, 0:wd], op=ALU.mult),
                           reads=[b_PS[2 + rc], b_RS], writes=[bol])

                def emit_oh(h):
                    ol, bol = OL[:, h % 2], b_OL[h % 2]
                    ps, bp = bank56()
                    matmul_group(ps[:, 0:wd], [(sluv[:, k * 2048 + h * 128: k * 2048 + h * 128 + 128], ol[:, k, 0:wd]) for k in range(2)],
                                 [[bsluv, bol]] * 2, bp)
                    act_copy(OH[:, h, 0:wd], ps[:, 0:wd], [bp], [b_OH[h]])

                q_heads(0)
                emit_qlat(0)
                vq = []
                ohq = []
                it = 0

                def pop_V():
                    info = vq.pop(0)
                    emit_V(info)
                    if info[1] == nkt - 1:
                        emit_fin(info[0])
                        ohq.append([info[0], 2])

                for h in range(16):
                    g, hh = divmod(h, 4)
                    if hh == 0 and g < 3:
                        q_heads(g + 1)
                    for kt in range(nkt):
                        vq.append(emit_S(h, kt, it))
                        if kt == 0 and h + 1 < 16:
                            emit_qlat(h + 1)
                        while vq and len(vq) > (3 if (vq[0][1] == 0 and h > 0) else 1):
                            pop_V()
                        for q_ in ohq:
                            q_[1] -= 1
                        while ohq and ohq[0][1] <= 0:
                            emit_oh(ohq.pop(0)[0])
                        it += 1
                while vq:
                    pop_V()
                while ohq:
                    emit_oh(ohq.pop(0)[0])
            else:
                for g in range(4):
                    q_heads(g)
                NQ = 16 * wd
                nchunk = (NQ + 511) // 512
                hpc = 16 // nchunk
                kbn, bkbn = KB[:, 0, :], b_KB[0]
                for ch in range(nchunk):
                    hs = slice(ch * hpc, (ch + 1) * hpc)
                    cw = hpc * wd
                    c0 = ch * cw
                    pss, bps = PS[ch % 2], b_PS[ch % 2]
                    matmul_group(pss[0:wd, 0:cw], [(KTN[:, 0, 0:wd], QLS[:, 0, hs, :]), (KTN[:, 1, 0:wd], QLS[:, 1, hs, :]), (KTN[0:64, 2, 0:wd], QR[0:64, hs, 0:wd])],
                                 [[b_KTN, b_QLS], [b_KTN, b_QLS], [b_KTN, b_QR[0]]], bps)
                    op('act', lambda e, pss=pss, c0=c0, cw=cw: e.activation(out=PN[0:wd, c0:c0 + cw], in_=pss[0:wd, 0:cw], func=AF.Exp, scale=SCALE), reads=[bps], writes=[b_PN])
                    op('dve', lambda e, c0=c0, cw=cw: e.tensor_tensor(out=PN[0:wd, c0:c0 + cw], in0=PN[0:wd, c0:c0 + cw], in1=MSKB[0:wd, c0:c0 + cw], op=ALU.mult),
                       reads=[b_PN, b_CB], writes=[b_PN])
                    for rc in range(2):
                        ps, bp = PS[2 + rc], b_PS[2 + rc]
                        matmul_group(ps[:, 0:cw], [(kbn[0:wd, rc * 128:(rc + 1) * 128], PN[0:wd, c0:c0 + cw])], [[bkbn, b_PN]], bp)
                        act_copy(ON[:, rc, c0:c0 + cw], ps[:, 0:cw], [bp], [b_ON])
                    ps, bp = PS[4], b_PS[4]
                    matmul_group(ps[:, 0:cw], [(ONESB[0:wd, :], PN[0:wd, c0:c0 + cw])], [[b_CB, b_PN]], bp)
                    act_copy(SNW[:, c0:c0 + cw], ps[:, 0:cw], [bp], [b_SNW])
                ONv = ON[:, :, :].rearrange("p r (h q) -> p r h q", q=wd)
                SNv = SNW[:, :].rearrange("p (h q) -> p h q", q=wd)
                hq3 = lambda ap: ap.rearrange("p (h q) -> p h q", q=DEC)
                cache_o = d_cache[o_]
                PTH = [PT[:, 0:384], PS[6][:, :].bitcast(BF16)[:, 0:384]]
                b_PTH = [b_PT, b_PS[6]]

                def emit_T(n):
                    b, pg = divmod(n, NPG)
                    kbp, bkbp = KBP[:, n % NGB, :], b_KBP[n % NGB]
                    col = b * NPG + pg
                    op('pool', lambda e, kbp=kbp, col=col: e.indirect_dma_start(out=kbp[:, 0:320], out_offset=None, in_=cache_o,
                                                                              in_offset=bass.IndirectOffsetOnAxis(ap=IDX[:, col:col + 1], axis=0)),
                       reads=[b_IDX], writes=[bkbp], dma=ds_g[n % NGB])
                    pth, bpth = PTH[n % 2], b_PTH[n % 2]
                    for c in range(3):
                        op('pe', lambda e, c=c, kbp=kbp, pth=pth: e.transpose(out=pth[:, c * 128:(c + 1) * 128], in_=kbp[:, c * 128:(c + 1) * 128], identity=IDB),
                           reads=[bkbp, b_CB], writes=[bpth])
                    ktp, bktp = KTP[:, n % 2, :], b_KTP[n % 2]
                    op('dve', lambda e, ktp=ktp, pth=pth: e.tensor_copy(out=ktp[:, 0:384], in_=pth[:, 0:384]), reads=[bpth], writes=[bktp])

                def emit_Ss(n):
                    b, pg = divmod(n, NPG)
                    qs = slice(b * DEC, (b + 1) * DEC)
                    ktp, bktp = KTP[:, n % 2, :], b_KTP[n % 2]
                    pss, bps = PS[n % 2], b_PS[n % 2]
                    matmul_group(hq3(pss[:, 0:HQ]),
                                 [(ktp[:, 0:128], QLS[:, 0, :, qs]), (ktp[:, 128:256], QLS[:, 1, :, qs]), (ktp[:, 256:384], QR[:, :, qs])],
                                 [[bktp, b_QLS], [bktp, b_QLS], [bktp, b_QR[0]]], bps)
                    pp, bpp = PP[:, n % 2, :], b_PP[n % 2]
                    op('act', lambda e, pss=pss, pp=pp: e.activation(out=pp, in_=pss[:, 0:HQ], func=AF.Exp, scale=SCALE), reads=[bps], writes=[bpp])

                def emit_Vs(n):
                    b, pg = divmod(n, NPG)
                    qs = slice(b * DEC, (b + 1) * DEC)
                    kbp, bkbp = KBP[:, n % NGB, :], b_KBP[n % NGB]
                    pp, bpp = PP[:, n % 2, :], b_PP[n % 2]
                    acc, bacc = PS[2 + (b % 2)], b_PS[2 + (b % 2)]
                    first, lastk = (pg == 0), (pg == NPG - 1)
                    for a in range(3):
                        lhs = kbp[:, a * 128:(a + 1) * 128] if a < 2 else ONESB
                        op('pe', lambda e, a=a, lhs=lhs, pp=pp, acc=acc, first=first, lastk=lastk: e.matmul(acc[:, a * HQ:(a + 1) * HQ], lhsT=lhs, rhs=pp, start=(first and a == 0), stop=(lastk and a == 2), skip_group_check=True),
                           reads=[bkbp, bpp, b_CB], writes=[bacc])
                    if lastk:
                        op('dve', lambda e, acc=acc, qs=qs: e.tensor_tensor(out=hq3(TS_[:, 2, :]), in0=hq3(acc[:, 2 * HQ:3 * HQ]), in1=SNv[:, :, qs], op=ALU.add),
                           reads=[bacc, b_SNW], writes=[b_TS])
                        op('dve', lambda e: e.reciprocal(out=TS_[:, 2, :], in_=TS_[:, 2, :]), reads=[b_TS], writes=[b_TS])
                        for rc in range(2):
                            op('dve', lambda e, acc=acc, qs=qs, rc=rc: e.tensor_tensor(out=hq3(TS_[:, rc, :]), in0=hq3(acc[:, rc * HQ:(rc + 1) * HQ]), in1=ONv[:, rc, :, qs], op=ALU.add),
                               reads=[bacc, b_ON], writes=[b_TS])
                            op('dve', lambda e, qs=qs, rc=rc: e.tensor_tensor(out=OLS[:, rc, :, qs], in0=hq3(TS_[:, rc, :]), in1=hq3(TS_[:, 2, :]), op=ALU.mult),
                               reads=[b_TS], writes=[b_OLS])

                NIT = NB * NPG
                emit_T(0)
                for n in range(NIT):
                    if n + 1 < NIT:
                        emit_T(n + 1)
                    emit_Ss(n)
                    if n >= 1:
                        emit_Vs(n - 1)
                emit_Vs(NIT - 1)
                for h in range(16):
                    ps, bp = bank56()
                    matmul_group(ps[:, 0:wd], [(sluv[:, k * 2048 + h * 128: k * 2048 + h * 128 + 128], OLS[:, k, h, :]) for k in range(2)],
                                 [[bsluv, b_OLS]] * 2, bp)
                    act_copy(OH[:, h, 0:wd], ps[:, 0:wd], [bp], [b_OH[h]])
            for g in range(4):
                sl, bsl = load_piece(f'O_out{o_}_{g}')
                for cc in range(2):
                    c = g * 2 + cc
                    ps, bp = proj_fm(sl, bsl, 256, cc * 128, OH, b_OH, 16, wd)
                    act_copy(YO[:, c, 0:wd], ps[:, 0:wd], [bp], [b_YO[c]])
            postnorm((i * 4 + 1) * 8, wd)
            sg.close()

        def zero_state():
            for e_ in range(2):
                op('dve', lambda e, e_=e_: e.memset(UH[e_][:, :, :], 0.0), writes=[b_UH[e_]])
                op('dve', lambda e, e_=e_: e.memset(VH[e_][:, :, :], 0.0), writes=[b_VH[e_]])
            for i in range(4):
                op('dve', lambda e, i=i: e.memset(GH[i][:, :, :], 0.0), writes=[b_GH[i]] + b_GHq[i])

        for seq in range(NSEQ):
            zero_state()
            for tq in range(NTS):
                c0 = tq * W
                for k in range(KD):
                    op('sp', lambda e, k=k, c0=c0, seq=seq: e.dma_start(out=X[:, k, :], in_=d_xp[seq, :, k, c0:c0 + W]), writes=[b_X[k]], dma=b_X[k].dsem)
                last = (tq == NTS - 1)
                for i in range(4):
                    if i % 2 == 0:
                        even(i // 2, 'p', W, tq == 0, last, i, seq)
                    else:
                        mla(i // 2, 'p', W, i, seq, tq)
                    ffn(i, 'p', W, seq, last)
                for k in range(KD):
                    op('sp', lambda e, k=k, c0=c0, seq=seq: e.dma_start(out=o_yp[seq, :, k, c0:c0 + W], in_=X[:, k, :]), reads=[b_X[k]], dma=b_X[k].dsem, store=True)

        for k in range(KD):
            op('sp', lambda e, k=k: e.dma_start(out=X[:, k, 0:WS], in_=d_xs[:, k, :]), writes=[b_X[k]], dma=b_X[k].dsem)
        for i in range(4):
            if i % 2 == 0:
                even(i // 2, 's', WS, False, True, i, 0)
            else:
                mla(i // 2, 's', WS, i, 0, 0)
            ffn(i, 's', WS, 0, True)
        for k in range(KD):
            op('sp', lambda e, k=k: e.dma_start(out=o_ys[:, k, :], in_=X[:, k, 0:WS]), reads=[b_X[k]], dma=b_X[k].dsem, store=True)

        S_.finish()
        print("ops per engine:", {e: len(S_.ops[e]) for e in ENGS}, "sbuf remaining", nc.sbuf_bytes_remaining, flush=True)
        with nc.Block() as block:
            S_.emit(block)
    return nc


def run(inp, cfg, trace=False):
    NC_, NSEQ, NB, S, WS = cfg.NCORES, cfg.NSEQ, cfg.NB, cfg.S, cfg.WS
    wst, _ = build_weight_stream(inp)
    pr = build_params(inp)
    kvg = np.ascontiguousarray(np.broadcast_to(inp['kv_norm'].reshape(1, 512), (128, 512))).astype(np.float32)
    cs = build_consts(cfg)
    cache = [np.concatenate([inp['cache_ckv'][o], inp['cache_krope'][o]], axis=-1).reshape(-1, 320) for o in range(2)]
    nc = build_nc(cfg)
    in_maps = []
    for c in range(NC_):
        xp = inp['x_prompt'][c * NSEQ:(c + 1) * NSEQ]
        xp = np.ascontiguousarray(xp.reshape(NSEQ, S, 8, 128).transpose(0, 3, 2, 1))
        xs = inp['x_sample'][c * NB:(c + 1) * NB].reshape(WS, 8, 128)
        xs = np.ascontiguousarray(xs.transpose(2, 1, 0))
        sp = inp['state_pool'][:, c * NB:(c + 1) * NB]
        sp = np.ascontiguousarray(sp.reshape(2, NB, 15, 4, 128).transpose(0, 4, 3, 1, 2))
        sc = inp['state_sconv'][:, c * NB:(c + 1) * NB]
        sc = np.ascontiguousarray(sc.reshape(2, NB, 2, 4, 128).transpose(0, 4, 3, 1, 2))
        sf = inp['state_ffn'][:, c * NB:(c + 1) * NB]
        sf = np.ascontiguousarray(sf.reshape(4, NB, 2, 22, 128).transpose(0, 4, 3, 1, 2))
        pt = inp['page_table'][c * NB:(c + 1) * NB].reshape(1, -1).astype(np.int32)
        ptb = np.ascontiguousarray(np.broadcast_to(pt, (128, pt.shape[1])))
        in_maps.append(dict(wst=wst, pr=pr, kvg=kvg, cs=cs, xp=xp, xs=xs, spool=sp, ssconv=sc, sffn=sf,
                            cache0=cache[0], cache1=cache[1], ptb=ptb))
    res = run_bass_kernel_spmd(nc, in_maps, core_ids=list(range(NC_)), **({'trace': True} if trace else {}))
    R = res.results
    cat = lambda f: np.concatenate([f(r) for r in R], axis=0)
    catb = lambda f, ax: np.concatenate([f(r) for r in R], axis=ax)
    y_p = cat(lambda r: r['o_yp'].transpose(0, 3, 2, 1).reshape(NSEQ, S, 1024))
    y_s = cat(lambda r: r['o_ys'].transpose(2, 1, 0).reshape(NB, cfg.DEC, 1024))
    pool_p = catb(lambda r: r['o_poolp'].transpose(0, 1, 4, 3, 2).reshape(2, NSEQ, 15, 512), 1)
    pool_s = catb(lambda r: r['o_pools'].transpose(0, 3, 4, 2, 1).reshape(2, NB, 15, 512), 1)
    sconv_p = catb(lambda r: r['o_sconvp'].transpose(0, 1, 4, 3, 2).reshape(2, NSEQ, 2, 512), 1)
    sconv_s = catb(lambda r: r['o_sconvs'].transpose(0, 3, 4, 2, 1).reshape(2, NB, 2, 512), 1)
    ckv_p = catb(lambda r: r['o_ckvp'], 1)
    kr_p = catb(lambda r: r['o_krp'], 1)
    ckv_s = catb(lambda r: r['o_ckvs'].reshape(2, NB, cfg.DEC, 256), 1)
    kr_s = catb(lambda r: r['o_krs'].reshape(2, NB, cfg.DEC, 64), 1)
    ffn_p = catb(lambda r: r['o_ffnp'].transpose(0, 1, 4, 3, 2).reshape(4, NSEQ, 2, 2816), 1)
    ffn_s = catb(lambda r: r['o_ffns'].transpose(0, 3, 4, 2, 1).reshape(4, NB, 2, 2816), 1)
    outs = (y_p, y_s, pool_p, pool_s, sconv_p, sconv_s, ckv_p, kr_p, ckv_s, kr_s, ffn_p, ffn_s)
    outs = tuple(np.ascontiguousarray(o).astype(np.float32) for o in outs)
    return outs, res


def kernel(**inputs):
    inp = {k: np.asarray(v) for k, v in inputs.items()}
    cfg = Cfg()
    outs, _ = run(inp, cfg)
    return outs
```
